# Optimizing a Trainium2 kernel written in Bass

```python
import jax
import jax.numpy as jnp
from jax import lax
import numpy as np

D_MODEL = 1024
BATCH = 8
SEQ = 2048
DEPTH = 1

MIX_WIDTH = D_MODEL
ATT_WIDTH = MIX_WIDTH // 2
RWKV_WIDTH = MIX_WIDTH - ATT_WIDTH
HEAD_DIM = 64
ATT_HEADS = ATT_WIDTH // HEAD_DIM
RWKV_HEAD_SIZE = 64
RWKV_HEADS = RWKV_WIDTH // RWKV_HEAD_SIZE
DILATED_PATTERNS = ((128, 1), (512, 4), (2048, 16))
ATT_BLOCK = 128
D_FF = ((8 * D_MODEL // 3 + 127) // 128) * 128
DECAY_LORA = max(32, int(round(1.8 * RWKV_WIDTH ** 0.5 / 32)) * 32)
ICL_LORA = max(32, int(round(1.8 * RWKV_WIDTH ** 0.5 / 32)) * 32)
GATE_LORA = max(32, int(round(0.6 * RWKV_WIDTH ** 0.8 / 32)) * 32)
IN_SPLITS = (ATT_WIDTH, ATT_WIDTH, ATT_WIDTH, RWKV_WIDTH, RWKV_WIDTH, RWKV_WIDTH, RWKV_WIDTH)
IN_COLS = sum(IN_SPLITS)
FFN_RESIDUAL = 0.5
RMS_EPS = 1e-6
GN_EPS = 64e-5
NEG_INF = -1e30

kernel_name = 'hymba_dilated_rwkv7_macaron'


def rms_norm(x, gain):
    xf = x.astype(jnp.float32)
    y = xf * lax.rsqrt(jnp.mean(xf * xf, axis=-1, keepdims=True) + RMS_EPS)
    return (y * gain.astype(jnp.float32)).astype(x.dtype)


def swiglu(h, w_gate, w_up, w_down):
    return (jax.nn.silu(h @ w_gate) * (h @ w_up)) @ w_down


def token_shift_lerp(x, mu):
    prev = jnp.pad(x, ((0, 0), (1, 0), (0, 0)))[:, :-1]
    return x + (prev - x) * mu


def banded_causal_attention(q, k, v, window):
    b, g, length, dh = q.shape
    nb = -(-length // ATT_BLOCK)
    lp = nb * ATT_BLOCK
    qb = jnp.pad(q, ((0, 0), (0, 0), (0, lp - length), (0, 0))).reshape(b, g, nb, ATT_BLOCK, dh)
    kv_pad = ((0, 0), (0, 0), (ATT_BLOCK, lp - length), (0, 0))
    kb = jnp.pad(k, kv_pad).reshape(b, g, nb + 1, ATT_BLOCK, dh)
    vb = jnp.pad(v, kv_pad).reshape(b, g, nb + 1, ATT_BLOCK, dh)
    kw = jnp.concatenate([kb[:, :, :-1], kb[:, :, 1:]], axis=3)
    vw = jnp.concatenate([vb[:, :, :-1], vb[:, :, 1:]], axis=3)
    s = jnp.einsum('bgnqd,bgnkd->bgnqk', qb, kw).astype(jnp.float32) * (dh ** -0.5)
    qi = jnp.arange(ATT_BLOCK)[:, None]
    kj = jnp.arange(2 * ATT_BLOCK)[None, :]
    dist = ATT_BLOCK + qi - kj
    kpos = (jnp.arange(nb)[:, None, None] - 1) * ATT_BLOCK + kj[None]
    mask = (dist >= 0) & (dist <= window) & (kpos >= 0)
    s = jnp.where(mask, s, NEG_INF)
    lse = jax.nn.logsumexp(s, axis=-1)
    p = jnp.exp(s - lse[..., None])
    o = jnp.einsum('bgnqk,bgnkd->bgnqd', p.astype(v.dtype), vw)
    return (o.reshape(b, g, lp, dh)[:, :, :length], lse.reshape(b, g, lp)[:, :, :length])


def dilated_attention(q, k, v):
    b, h, s, dh = q.shape
    outs, lses = [], []
    for window, dil in DILATED_PATTERNS:
        length = s // dil

        def to_sub(t):
            return t.reshape(b, h, length, dil, dh).transpose(0, 1, 3, 2, 4).reshape(b, h * dil, length, dh)

        o, lse = banded_causal_attention(to_sub(q), to_sub(k), to_sub(v), window // dil)
        outs.append(o.reshape(b, h, dil, length, dh).transpose(0, 1, 3, 2, 4).reshape(b, h, s, dh))
        lses.append(lse.reshape(b, h, dil, length).transpose(0, 1, 3, 2).reshape(b, h, s))
    wts = jax.nn.softmax(jnp.stack(lses), axis=0)
    o = jnp.sum(wts[..., None] * jnp.stack(outs).astype(jnp.float32), axis=0)
    return o.astype(q.dtype)


def rwkv7_scan(r, decay, k, v, a, b):
    bsz, _, nh, n = r.shape

    def step(state, inp):
        r_t, w_t, k_t, v_t, a_t, b_t = inp
        sa = jnp.einsum('bhij,bhj->bhi', state, a_t)
        state = state * w_t[:, :, None, :] + sa[..., None] * b_t[:, :, None, :] + v_t[..., None] * k_t[:, :, None, :]
        return state, jnp.einsum('bhij,bhj->bhi', state, r_t)

    xs = tuple(jnp.moveaxis(t, 1, 0) for t in (r, decay, k, v, a, b))
    s0 = jnp.zeros((bsz, nh, n, n), jnp.float32)
    _, ys = lax.scan(step, s0, xs)
    return jnp.moveaxis(ys, 0, 1)


def rwkv7_time_mix(r_in, k_in, v_in, c_in, mu_r, mu_k, mu_v, mu_w, mu_a, mu_g,
                   w0, w1, w2, a0, a1, a2, g1, g2, k_k, k_a, r_k, ln_x_w, ln_x_b):
    bsz, s, c = r_in.shape
    f32 = jnp.float32

    def heads(t):
        return t.astype(f32).reshape(bsz, s, RWKV_HEADS, RWKV_HEAD_SIZE)

    r = token_shift_lerp(r_in, mu_r)
    k = token_shift_lerp(k_in, mu_k)
    v = token_shift_lerp(v_in, mu_v)
    cw = token_shift_lerp(c_in, mu_w)
    ca = token_shift_lerp(c_in, mu_a)
    cg = token_shift_lerp(c_in, mu_g)
    w_log = -jax.nn.softplus(-(w0 + jnp.tanh(cw @ w1) @ w2).astype(f32)) - 0.5
    decay = jnp.exp(-jnp.exp(w_log))
    a = jax.nn.sigmoid((a0 + (ca @ a1) @ a2).astype(f32))
    g = jax.nn.sigmoid(cg @ g1) @ g2
    kk = heads(k * k_k)
    kk = kk / jnp.maximum(jnp.linalg.norm(kk, axis=-1, keepdims=True), 1e-12)
    k = k.astype(f32) * (1.0 + (a - 1.0) * k_a.astype(f32))
    rh, kh, vh = heads(r), heads(k), heads(v)
    y = rwkv7_scan(rh, heads(decay), kh, vh, -kk, kk * heads(a))
    mean = jnp.mean(y, axis=-1, keepdims=True)
    var = jnp.mean(jnp.square(y - mean), axis=-1, keepdims=True)
    y = ((y - mean) * lax.rsqrt(var + GN_EPS)).reshape(bsz, s, c) * ln_x_w.astype(f32) + ln_x_b.astype(f32)
    bonus = jnp.sum(rh * kh * r_k.astype(f32), axis=-1, keepdims=True) * vh
    out = (y + bonus.reshape(bsz, s, c)) * g.astype(f32)
    return out.astype(r_in.dtype)


def setup_inputs(seed: int = 0) -> dict:
    key = jax.random.key(seed)
    ks = iter(jax.random.split(key, 40))
    f32 = jnp.float32

    def normal(shape, scale):
        return jax.random.normal(next(ks), shape, f32) * scale

    def gain(shape):
        return 1.0 + 0.02 * jax.random.normal(next(ks), shape, f32)

    def unif(shape, lo, hi):
        return jax.random.uniform(next(ks), shape, f32, lo, hi)

    L, D, RW = DEPTH, D_MODEL, RWKV_WIDTH
    return {
        'x': jax.random.normal(next(ks), (BATCH, SEQ, D), f32),
        'ffn1_norm': gain((L, D)),
        'ffn1_w_gate': normal((L, D, D_FF), D ** -0.5),
        'ffn1_w_up': normal((L, D, D_FF), D ** -0.5),
        'ffn1_w_down': normal((L, D_FF, D), D_FF ** -0.5),
        'mix_norm': gain((L, D)),
        'w_in': normal((L, D, IN_COLS), D ** -0.5),
        'q_norm': gain((L, HEAD_DIM)),
        'k_norm': gain((L, HEAD_DIM)),
        'mu_r': unif((L, RW), 0.0, 1.0),
        'mu_k': unif((L, RW), 0.0, 1.0),
        'mu_v': unif((L, RW), 0.0, 1.0),
        'mu_w': unif((L, RW), 0.0, 1.0),
        'mu_a': unif((L, RW), 0.0, 1.0),
        'mu_g': unif((L, RW), 0.0, 1.0),
        'w0': unif((L, RW), -6.5, -1.0),
        'w1': normal((L, RW, DECAY_LORA), RW ** -0.5),
        'w2': normal((L, DECAY_LORA, RW), 0.1 * DECAY_LORA ** -0.5),
        'a0': normal((L, RW), 0.1),
        'a1': normal((L, RW, ICL_LORA), RW ** -0.5),
        'a2': normal((L, ICL_LORA, RW), 0.1 * ICL_LORA ** -0.5),
        'g1': normal((L, RW, GATE_LORA), RW ** -0.5),
        'g2': normal((L, GATE_LORA, RW), GATE_LORA ** -0.5),
        'k_k': 0.85 + normal((L, RW), 0.05),
        'k_a': 1.0 + normal((L, RW), 0.05),
        'r_k': -0.04 + normal((L, RWKV_HEADS, RWKV_HEAD_SIZE), 0.02),
        'ln_x_w': gain((L, RW)),
        'ln_x_b': normal((L, RW), 0.02),
        'w_out': normal((L, MIX_WIDTH, D), MIX_WIDTH ** -0.5),
        'ffn2_norm': gain((L, D)),
        'ffn2_w_gate': normal((L, D, D_FF), D ** -0.5),
        'ffn2_w_up': normal((L, D, D_FF), D ** -0.5),
        'ffn2_w_down': normal((L, D_FF, D), D_FF ** -0.5),
    }


def reference(x, ffn1_norm, ffn1_w_gate, ffn1_w_up, ffn1_w_down, mix_norm, w_in, q_norm, k_norm,
              mu_r, mu_k, mu_v, mu_w, mu_a, mu_g, w0, w1, w2, a0, a1, a2, g1, g2,
              k_k, k_a, r_k, ln_x_w, ln_x_b, w_out, ffn2_norm, ffn2_w_gate, ffn2_w_up, ffn2_w_down):
    b, s, _ = x.shape
    split_idx = [int(i) for i in np.cumsum(IN_SPLITS)[:-1]]
    for l in range(DEPTH):
        h = rms_norm(x, ffn1_norm[l])
        x = x + FFN_RESIDUAL * swiglu(h, ffn1_w_gate[l], ffn1_w_up[l], ffn1_w_down[l])
        h = rms_norm(x, mix_norm[l])
        proj = h @ w_in[l]
        q, k, v, rr, rk, rv, rc = jnp.split(proj, split_idx, axis=-1)
        qh = rms_norm(q.reshape(b, s, ATT_HEADS, HEAD_DIM), q_norm[l]).transpose(0, 2, 1, 3)
        kh = rms_norm(k.reshape(b, s, ATT_HEADS, HEAD_DIM), k_norm[l]).transpose(0, 2, 1, 3)
        vh = v.reshape(b, s, ATT_HEADS, HEAD_DIM).transpose(0, 2, 1, 3)
        att = dilated_attention(qh, kh, vh).transpose(0, 2, 1, 3).reshape(b, s, ATT_WIDTH)
        rw = rwkv7_time_mix(rr, rk, rv, rc, mu_r[l], mu_k[l], mu_v[l], mu_w[l], mu_a[l], mu_g[l],
                            w0[l], w1[l], w2[l], a0[l], a1[l], a2[l], g1[l], g2[l],
                            k_k[l], k_a[l], r_k[l], ln_x_w[l], ln_x_b[l])
        x = x + jnp.concatenate([att, rw], axis=-1) @ w_out[l]
        h = rms_norm(x, ffn2_norm[l])
        x = x + FFN_RESIDUAL * swiglu(h, ffn2_w_gate[l], ffn2_w_up[l], ffn2_w_down[l])
    return x
```

```python
import contextlib
import numpy as np
import concourse.bass as bass
import concourse.mybir as mybir
from concourse.bass_utils import run_bass_kernel_spmd

F32 = mybir.dt.float32
BF16 = mybir.dt.bfloat16
AF = mybir.ActivationFunctionType
ALU = mybir.AluOpType

S = 2048
D = 1024
DFF = 2816
NFC = 22
FGROUPS = [(0, 8), (8, 7), (15, 7)]
NSLOT = 8
NTILES = 66 + 12 + 4 + 12 + 8 + 66
RMS_EPS = 1e-6
GN_EPS = 64e-5
TS = 256
NCH = 4
CH = 64

C_ID, C_ONES, C_BD, C_MB, C_SU, C_SL, C_I4, C_UI = 0, 128, 256, 384, 640, 1152, 1664, 2176
NC_COLS = 2432
P_G1, P_GM, P_G2, P_QG, P_KG = 0, 8, 16, 24, 25
P_MUR, P_MUK, P_MUV, P_MUW, P_MUA, P_MUG = 26, 30, 34, 38, 42, 46
P_W0, P_A0, P_KK, P_KA, P_RK, P_LNW, P_LNB = 50, 54, 58, 62, 66, 70, 74
NP_COLS = 78
L_W1, L_A1, L_G1, L_W2, L_A2, L_G2 = 0, 128, 256, 640, 1152, 1664
NL_COLS = 2176


class Ctx:
    def __init__(self, nc, stack):
        self.nc = nc
        self.stack = stack
        self.eng = {'pe': nc.tensor, 'act': nc.scalar, 'dve': nc.vector, 'pool': nc.gpsimd, 'sp': nc.sync}
        self.sem = {e: stack.enter_context(nc.semaphore("sem_" + e)) for e in ('pe', 'act', 'dve')}
        self.cnt = {e: 0 for e in ('pe', 'act', 'dve')}
        self.dsem = {}
        self.dcnt = {}
        self.last_w = {}
        self.readers = {}
        self.waited = {}
        self.barrier_tokens = []
        self.nops = 0

    def dma_sem(self, name):
        if name not in self.dsem:
            self.dsem[name] = self.stack.enter_context(self.nc.semaphore("dsem_" + name))
            self.dcnt[name] = 0
        return name

    def barrier(self):
        toks = []
        for e in ('pe', 'act', 'dve'):
            if self.cnt[e] > 0:
                toks.append((e, self.sem[e], self.cnt[e], e))
        for n in self.dsem:
            if self.dcnt[n] > 0:
                toks.append((n, self.dsem[n], self.dcnt[n], 'dma'))
        self.barrier_tokens = toks

    def op(self, eng, fn, reads=(), writes=(), dma=None):
        toks = []
        for k in reads:
            t = self.last_w.get(k)
            if t is not None:
                toks.append((t, True))
        for k in writes:
            t = self.last_w.get(k)
            if t is not None:
                toks.append((t, False))
            for t in self.readers.get(k, {}).values():
                toks.append((t, False))
        for t in self.barrier_tokens:
            toks.append((t, True))
        need = {}
        for (name, h, val, peng), is_raw in toks:
            if peng == eng and eng == 'pe':
                continue
            if name not in need or need[name][1] < val:
                need[name] = (h, val)
        e = self.eng[eng]
        for name, (h, val) in need.items():
            if self.waited.get((eng, name), 0) >= val:
                continue
            e.wait_ge(h, val)
            self.waited[(eng, name)] = val
        if fn is None:
            return
        ins = fn(e)
        self.nops += 1
        if dma is not None:
            self.dcnt[dma] += 16
            ins.then_inc(self.dsem[dma], 16)
            tok = (dma, self.dsem[dma], self.dcnt[dma], 'dma')
        else:
            self.cnt[eng] += 1
            ins.then_inc(self.sem[eng], 1)
            tok = (eng, self.sem[eng], self.cnt[eng], eng)
        for k in reads:
            self.readers.setdefault(k, {})[tok[0]] = tok
        for k in writes:
            self.last_w[k] = tok
            self.readers[k] = {}

    def mm(self, out, lhsT, rhs, start, stop, reads, writes):
        self.op('pe', lambda e: e.matmul(out, lhsT=lhsT, rhs=rhs, start=start, stop=stop), reads, writes)

    def act(self, out, in_, func, reads, writes, bias=None, scale=None):
        kw = {}
        if bias is not None:
            kw['bias'] = bias
        if scale is not None:
            kw['scale'] = scale
        self.op('act', lambda e: e.activation(out=out, in_=in_, func=func, **kw), reads, writes)

    def tt(self, out, in0, in1, op, reads, writes):
        self.op('dve', lambda e: e.tensor_tensor(out=out, in0=in0, in1=in1, op=op), reads, writes)

    def ts(self, out, in0, s1, s2, op0, op1, reads, writes):
        if s2 is None:
            self.op('dve', lambda e: e.tensor_scalar(out=out, in0=in0, scalar1=s1, scalar2=None, op0=op0), reads, writes)
        else:
            self.op('dve', lambda e: e.tensor_scalar(out=out, in0=in0, scalar1=s1, scalar2=s2, op0=op0, op1=op1), reads, writes)

    def stt(self, out, in0, scalar, in1, op0, op1, reads, writes):
        self.op('dve', lambda e: e.scalar_tensor_tensor(out=out, in0=in0, scalar=scalar, in1=in1, op0=op0, op1=op1),
                reads, writes)

    def copy(self, eng, out, in_, reads, writes):
        if eng == 'act':
            self.act(out, in_, AF.Copy, reads, writes)
        else:
            self.op('dve', lambda e: e.tensor_copy(out=out, in_=in_), reads, writes)

    def memset(self, ap, val, writes):
        self.op('dve', lambda e: e.memset(ap, val), (), writes)


def build_nc(stage=99, taps=()):
    nc = bass.Bass("TRN2", target_bir_lowering=False)
    xT_d = nc.dram_tensor("xT", [D, S], F32, kind="ExternalInput").ap()
    w_d = nc.dram_tensor("wst", [NTILES, 128, 1024], F32, kind="ExternalInput").ap()
    c_d = nc.dram_tensor("consts", [128, NC_COLS], F32, kind="ExternalInput").ap()
    p_d = nc.dram_tensor("params", [128, NP_COLS], F32, kind="ExternalInput").ap()
    l_d = nc.dram_tensor("lora", [128, NL_COLS], F32, kind="ExternalInput").ap()
    yT_d = nc.dram_tensor("yT", [D, S], F32, kind="ExternalOutput").ap()
    tap_d = {}
    for name, shape, dt in taps:
        tap_d[name] = nc.dram_tensor("tap_" + name, list(shape), dt, kind="ExternalOutput").ap()

    with contextlib.ExitStack() as st:
        cx = Ctx(nc, st)

        uniq = [0]

        def sb(name, shape, dt, stack=st):
            uniq[0] += 1
            return stack.enter_context(nc.sbuf_tensor("%s_%d" % (name, uniq[0]), list(shape), dt))

        x = sb("x", [128, 8, S], F32)
        hT = sb("hT", [128, 8, S], BF16)
        ring = sb("ring", [128, NSLOT, 1024], BF16)
        cb = sb("cb", [128, NC_COLS], BF16)
        pf = sb("pf", [128, NP_COLS], F32)
        pf2 = sb("pf2", [128, 32], F32)
        lb = sb("lb", [128, NL_COLS], BF16)
        ps = [st.enter_context(nc.psum_tensor("ps%d" % i, [128, 512], F32)) for i in range(8)]

        ident = cb[:, C_ID:C_ID + 128]
        ones = cb[:, C_ONES:C_ONES + 128]
        bdones = cb[:, C_BD:C_BD + 128]

        wstate = {'next': 0}
        for i in range(NSLOT):
            cx.dma_sem("w%d" % i)

        def wtile(dst=None, dst_key=None, sem=None):
            i = wstate['next']
            wstate['next'] += 1
            if dst is None:
                slot = wstate.get('ring', 0) % NSLOT
                wstate['ring'] = wstate.get('ring', 0) + 1
                dst = ring[:, slot, :]
                dst_key = "ring%d" % slot
                sem = "w%d" % slot
            cx.op('pool', lambda e: e.dma_start(out=dst, in_=w_d[i, :, :]), reads=(), writes=(dst_key,), dma=sem)
            return dst, dst_key

        cx.dma_sem("init")
        cx.dma_sem("init2")
        cx.dma_sem("inith")
        cx.op('pool', lambda e: e.dma_start(out=cb[:], in_=c_d[:, :]), (), ("cb",), dma="init")
        cx.op('pool', lambda e: e.dma_start(out=lb[:], in_=l_d[:, :]), (), ("lb",), dma="init2")
        cx.op('sp', lambda e: e.dma_start(out=pf[:], in_=p_d[:, :]), (), ("pf",), dma="inith")
        xv = xT_d.rearrange("(k p) t -> p k t", p=128)
        for t4 in range(4):
            nm = cx.dma_sem("x%d" % t4)
            cx.op('sp', lambda e: e.dma_start(out=x[:, :, t4 * 512:(t4 + 1) * 512], in_=xv[:, :, t4 * 512:(t4 + 1) * 512]),
                  (), tuple(("x", k, t4) for k in range(8)), dma=nm)
        cx.ts(pf2[:, 0:24], pf[:, P_MUR:P_MUR + 24], -1.0, 1.0, ALU.mult, ALU.add, ("pf",), ("pf2",))
        cx.ts(pf2[:, 24:28], pf[:, P_KA:P_KA + 4], -1.0, 1.0, ALU.mult, ALU.add, ("pf",), ("pf2",))
        cx.barrier()

        tapsem = {}

        def tap(name, src_ap, keys):
            if name not in tap_d:
                return
            nm = cx.dma_sem("tap_" + name)
            cx.op('sp', lambda e: e.dma_start(out=tap_d[name], in_=src_ap), keys, ("tapout_" + name,), dma=nm)

        def rmsnorm(gcol):
          with contextlib.ExitStack() as scope:
            sq = sb("n_sq", [128, 2, 512], BF16, scope)
            lnv = sb("n_lnv", [128, 512], F32, scope)
            rstd = sb("n_rstd", [128, 2, 512], F32, scope)
            for t4 in range(4):
                tsl = slice(t4 * 512, (t4 + 1) * 512)
                pst = ps[6 + (t4 % 2)]
                pk = "ps%d" % (6 + (t4 % 2))
                for k in range(8):
                    cx.act(sq[:, k % 2, :], x[:, k, tsl], AF.Square, (("x", k, t4),), (("n_sq", k % 2),))
                    cx.mm(pst[:, :], ones, sq[:, k % 2, :], k == 0, k == 7, (("n_sq", k % 2),), (pk,))
                cx.act(lnv[:], pst[:, :], AF.Ln, (pk,), ("n_lnv",), bias=RMS_EPS, scale=1.0 / D)
                cx.act(rstd[:, t4 % 2, :], lnv[:], AF.Exp, ("n_lnv",), (("n_rstd", t4 % 2),), scale=-0.5)
                for k in range(8):
                    cx.stt(hT[:, k, tsl], x[:, k, tsl], pf[:, gcol + k:gcol + k + 1], rstd[:, t4 % 2, :], ALU.mult, ALU.mult,
                           (("x", k, t4), ("n_rstd", t4 % 2)), (("hT", k, t4),))
          cx.barrier()

        def ffn(gcol):
            rmsnorm(gcol)
            with contextlib.ExitStack() as sc:
                AT = sb("f_AT", [128, 8, S], BF16, sc)
                wd = sb("f_wd", [128, 8, 1024], BF16, sc)
                sg = sb("f_sg", [128, 2, 512], F32, sc)
                it = 0
                for (f0, nf) in FGROUPS:
                    for c in range(nf):
                        tg, kg = wtile()
                        tu, ku = wtile()
                        for t4 in range(4):
                            tsl = slice(t4 * 512, (t4 + 1) * 512)
                            b = it % 2
                            it += 1
                            pG, pU = ps[b], ps[2 + b]
                            kG, kU = "ps%d" % b, "ps%d" % (2 + b)
                            for k in range(8):
                                cx.mm(pG[:, :], tg[:, k * 128:(k + 1) * 128], hT[:, k, tsl], k == 0, k == 7,
                                      (kg, ("hT", k, t4)), (kG,))
                            for k in range(8):
                                cx.mm(pU[:, :], tu[:, k * 128:(k + 1) * 128], hT[:, k, tsl], k == 0, k == 7,
                                      (ku, ("hT", k, t4)), (kU,))
                            cx.act(sg[:, b, :], pG[:, :], AF.Silu, (kG,), (("f_sg", b),))
                            cx.tt(AT[:, c, tsl], sg[:, b, :], pU[:, :], ALU.mult, (("f_sg", b), kU), (("f_AT", c, t4),))
                    for c in range(nf):
                        wtile(dst=wd[:, c, :], dst_key=("f_wd", c), sem=cx.dma_sem("wd%d" % c))
                    jt = 0
                    for dc in range(8):
                        for t4 in range(4):
                            tsl = slice(t4 * 512, (t4 + 1) * 512)
                            b = jt % 2
                            jt += 1
                            pD, kD = ps[4 + b], "ps%d" % (4 + b)
                            for c in range(nf):
                                cx.mm(pD[:, :], wd[:, c, dc * 128:(dc + 1) * 128], AT[:, c, tsl], c == 0, c == nf - 1,
                                      (("f_wd", c), ("f_AT", c, t4)), (kD,))
                            cx.stt(x[:, dc, tsl], pD[:, :], 0.5, x[:, dc, tsl], ALU.mult, ALU.add,
                                   (kD, ("x", dc, t4)), (("x", dc, t4),))
            cx.barrier()

        ffn(P_G1)
        tap("x1", x[:], tuple(("x", k, t) for k in range(8) for t in range(4)))

        if stage >= 2:
            mixer(nc, cx, sb, ps, x, hT, cb, pf, pf2, lb, wtile, rmsnorm, tap, stage)
        if stage >= 4:
            ffn(P_G2)

        yv = yT_d.rearrange("(k p) t -> p k t", p=128)
        cx.dma_sem("out")
        for t4 in range(4):
            cx.op('sp', lambda e: e.dma_start(out=yv[:, :, t4 * 512:(t4 + 1) * 512], in_=x[:, :, t4 * 512:(t4 + 1) * 512]),
                  tuple(("x", k, t4) for k in range(8)), ("yout",), dma="out")
        cx.barrier()
        cx.op('sp', None)
        cx.op('pe', None)
        cx.op('act', None)
        cx.op('dve', None)
        cx.op('pool', None)
    return nc


def mixer(nc, cx, sb, ps, x, hT, cb, pf, pf2, lb, wtile, rmsnorm, tap, stage):
    ident = cb[:, C_ID:C_ID + 128]
    ones = cb[:, C_ONES:C_ONES + 128]
    bdones = cb[:, C_BD:C_BD + 128]
    maskb = cb[:, C_MB:C_MB + 256]
    SU4 = cb[:, C_SU:C_SU + 512]
    SL4 = cb[:, C_SL:C_SL + 512]
    I4 = cb[:, C_I4:C_I4 + 512]
    UI4 = cb[:, C_UI:C_UI + 256]

    rmsnorm(P_GM)
    with contextlib.ExitStack() as msc:
        mix = sb("m_mix", [128, 8, S], BF16, msc)

        with contextlib.ExitStack() as asc:
            qkv = sb("a_qkv", [128, 3, S], BF16, asc)
            sqb = sb("a_sq", [128, 2, 512], BF16, asc)
            lnv = sb("a_lnv", [128, 512], F32, asc)
            rst = sb("a_rst", [128, 2, 512], F32, asc)
            accN = sb("a_accN", [128, S], F32, asc)
            accD = sb("a_accD", [128, S], F32, asc)
            Vt = sb("a_Vt", [128, 3, 128], BF16, asc)
            Pt = sb("a_Pt", [128, 2, 512], BF16, asc)
            it = 0
            for hp in range(4):
                for j in range(3):
                    tw, kw = wtile()
                    for t4 in range(4):
                        tsl = slice(t4 * 512, (t4 + 1) * 512)
                        b = it % 2
                        it += 1
                        pA, kA = ps[b], "ps%d" % b
                        for k in range(8):
                            cx.mm(pA[:, :], tw[:, k * 128:(k + 1) * 128], hT[:, k, tsl], k == 0, k == 7,
                                  (kw, ("hT", k, t4)), (kA,))
                        if j == 2:
                            cx.act(qkv[:, 2, tsl], pA[:, :], AF.Copy, (kA,), (("a_qkv", 2, t4),))
                        else:
                            pB, kB = ps[2 + b], "ps%d" % (2 + b)
                            cx.act(sqb[:, b, :], pA[:, :], AF.Square, (kA,), (("a_sq", b),))
                            cx.mm(pB[:, :], bdones, sqb[:, b, :], True, True, (("a_sq", b),), (kB,))
                            cx.act(lnv[:], pB[:, :], AF.Ln, (kB,), ("a_lnv",), bias=RMS_EPS, scale=1.0 / 64)
                            cx.act(rst[:, b, :], lnv[:], AF.Exp, ("a_lnv",), (("a_rst", b),), scale=-0.5)
                            gc = P_QG if j == 0 else P_KG
                            cx.stt(qkv[:, j, tsl], pA[:, :], pf[:, gc:gc + 1], rst[:, b, :], ALU.mult, ALU.mult,
                                   (kA, ("a_rst", b)), (("a_qkv", j, t4),))
                if hp == 0:
                    tap("q0", qkv[:, 0, :], tuple(("a_qkv", 0, t) for t in range(4)))
                allq = tuple(("a_qkv", j, t) for j in range(3) for t in range(4))
                ui = 0
                for pi, dil in enumerate((1, 4, 16)):
                    nb = 16 // dil
                    for c in range(dil):
                        for n in range(nb):
                            def toks(nn):
                                s0 = c + dil * 128 * nn
                                return slice(s0, s0 + dil * 127 + 1, dil)
                            tq = toks(n)
                            lo = (c + dil * 128 * n) // 512
                            hi = (c + dil * 128 * n + dil * 127) // 512
                            acck = tuple(("a_acc", t) for t in range(lo, hi + 1))
                            vs = ui % 3
                            vp = (ui - 1) % 3
                            b = ui % 2
                            ui += 1
                            pV, kV = ps[4 + b], "ps%d" % (4 + b)
                            cx.mm(pV[:, 0:128], qkv[:, 2, tq], ident, True, True, allq, (kV,))
                            cx.act(Vt[:, vs, :], pV[:, 0:128], AF.Copy, (kV,), (("a_Vt", vs),))
                            pS, kS = ps[b], "ps%d" % b
                            pO, kO = ps[2 + b], "ps%d" % (2 + b)
                            for e in range(2):
                                hs = slice(64 * e, 64 * e + 64)
                                base = 256 * e
                                if n >= 1:
                                    tk = toks(n - 1)
                                    cx.mm(pS[:, base:base + 128], ident, maskb[:, 0:128], True, False, (), (kS,))
                                    cx.mm(pS[:, base:base + 128], qkv[hs, 1, tk], qkv[hs, 0, tq], False, True, allq, (kS,))
                                    cx.mm(pS[:, base + 128:base + 256], ident, maskb[:, 128:256], True, False, (), (kS,))
                                    cx.mm(pS[:, base + 128:base + 256], qkv[hs, 1, tq], qkv[hs, 0, tq], False, True, allq, (kS,))
                                else:
                                    cx.mm(pS[:, base + 128:base + 256], ident, maskb[:, 128:256], True, False, (), (kS,))
                                    cx.mm(pS[:, base + 128:base + 256], qkv[hs, 1, tq], qkv[hs, 0, tq], False, True, allq, (kS,))
                            if n >= 1:
                                cx.act(Pt[:, b, :], pS[:, :], AF.Exp, (kS,), (("a_Pt", b),), scale=0.125)
                            else:
                                for e in range(2):
                                    cx.act(Pt[:, b, 256 * e + 128:256 * e + 256], pS[:, 256 * e + 128:256 * e + 256], AF.Exp,
                                           (kS,), (("a_Pt", b),), scale=0.125)
                            for e in range(2):
                                base = 256 * e
                                for (lh, lk, oc) in ((None, None, 0), (ones, None, 256)):
                                    osl = slice(oc + 128 * e, oc + 128 * e + 128)
                                    if n >= 1:
                                        l0 = Vt[:, vp, :] if lh is None else lh
                                        l1 = Vt[:, vs, :] if lh is None else lh
                                        cx.mm(pO[:, osl], l0, Pt[:, b, base:base + 128], True, False,
                                              (("a_Vt", vp), ("a_Pt", b)), (kO,))
                                        cx.mm(pO[:, osl], l1, Pt[:, b, base + 128:base + 256], False, True,
                                              (("a_Vt", vs), ("a_Pt", b)), (kO,))
                                    else:
                                        l1 = Vt[:, vs, :] if lh is None else lh
                                        cx.mm(pO[:, osl], l1, Pt[:, b, base + 128:base + 256], True, True,
                                              (("a_Vt", vs), ("a_Pt", b)), (kO,))
                            for e in range(2):
                                hs = slice(64 * e, 64 * e + 64)
                                for (acc, oc) in ((accN, 0), (accD, 256)):
                                    src = pO[hs, oc + 128 * e:oc + 128 * e + 128]
                                    if pi == 0:
                                        cx.copy('dve', acc[hs, tq], src, (kO,), acck)
                                    else:
                                        cx.tt(acc[hs, tq], acc[hs, tq], src, ALU.add, (kO,) + acck, acck)
                for t4 in range(4):
                    tsl = slice(t4 * 512, (t4 + 1) * 512)
                    cx.op('dve', lambda e: e.reciprocal(out=accD[:, tsl], in_=accD[:, tsl]), (("a_acc", t4),), (("a_acc", t4),))
                    cx.tt(mix[:, hp, tsl], accN[:, tsl], accD[:, tsl], ALU.mult, (("a_acc", t4),), (("m_mix", hp, t4),))
            tap("att", mix[:, 0:4, :], tuple(("m_mix", k, t) for k in range(4) for t in range(4)))
        cx.barrier()
        if stage < 3:
            for k in range(4, 8):
                for t4 in range(4):
                    cx.memset(mix[:, k, t4 * 512:(t4 + 1) * 512], 0.0, (("m_mix", k, t4),))
            for _ in range(16):
                wtile()
        else:
            rwkv(nc, cx, sb, ps, hT, cb, pf, pf2, lb, wtile, tap, mix, msc)

        it = 0
        for dc in range(8):
            tw, kw = wtile()
            for t4 in range(4):
                tsl = slice(t4 * 512, (t4 + 1) * 512)
                b = it % 2
                it += 1
                pA, kA = ps[b], "ps%d" % b
                for k in range(8):
                    cx.mm(pA[:, :], tw[:, k * 128:(k + 1) * 128], mix[:, k, tsl], k == 0, k == 7,
                          (kw, ("m_mix", k, t4)), (kA,))
                cx.tt(x[:, dc, tsl], x[:, dc, tsl], pA[:, :], ALU.add, (kA, ("x", dc, t4)), (("x", dc, t4),))
    cx.barrier()
    tap("x2", x[:], tuple(("x", k, t) for k in range(8) for t in range(4)))


def rwkv(nc, cx, sb, ps, hT, cb, pf, pf2, lb, wtile, tap, mix, msc):
    ident = cb[:, C_ID:C_ID + 128]
    bdones = cb[:, C_BD:C_BD + 128]
    SU4 = cb[:, C_SU:C_SU + 512]
    SL4 = cb[:, C_SL:C_SL + 512]
    I4 = cb[:, C_I4:C_I4 + 512]
    UI4 = cb[:, C_UI:C_UI + 256]
    dw = sb("r_dw", [128, S], BF16, msc)
    da = sb("r_da", [128, S], BF16, msc)
    dg = sb("r_dg", [128, S], BF16, msc)

    with contextlib.ExitStack() as csc:
        Rc = sb("c_R", [128, 4, 513], F32, csc)
        cl = sb("c_l", [128, 4, 512], BF16, csc)
        tmp = sb("c_tmp", [128, 512], F32, csc)
        ctiles = [wtile() for _ in range(4)]
        for k in range(4):
            cx.memset(Rc[:, k, 512:513], 0.0, (("c_R", k),))
        it = 0
        for t4 in range(4):
            tsl = slice(t4 * 512, (t4 + 1) * 512)
            for kc in range(4):
                tw, kw = ctiles[kc]
                b = it % 2
                it += 1
                pA, kA = ps[b], "ps%d" % b
                for k in range(8):
                    cx.mm(pA[:, :], tw[:, k * 128:(k + 1) * 128], hT[:, k, tsl], k == 0, k == 7, (kw, ("hT", k, t4)), (kA,))
                cx.copy('dve', Rc[:, kc, 0:1], Rc[:, kc, 512:513], (("c_R", kc),), (("c_R", kc),))
                cx.act(Rc[:, kc, 1:513], pA[:, :], AF.Copy, (kA,), (("c_R", kc),))
            for li, (mucol, omcol, lcol, ncol, dst, func) in enumerate((
                    (P_MUW, 12, L_W1, 32, dw, AF.Tanh), (P_MUA, 16, L_A1, 32, da, AF.Copy), (P_MUG, 20, L_G1, 96, dg, AF.Sigmoid))):
                for kc in range(4):
                    cx.ts(tmp[:], Rc[:, kc, 0:512], pf[:, mucol + kc:mucol + kc + 1], None, ALU.mult, None,
                          (("c_R", kc),), ("c_tmp",))
                    cx.stt(cl[:, kc, :], Rc[:, kc, 1:513], pf2[:, omcol + kc:omcol + kc + 1], tmp[:], ALU.mult, ALU.add,
                           (("c_R", kc), "c_tmp"), (("c_l", kc),))
                pB, kB = ps[2 + (li % 2)], "ps%d" % (2 + (li % 2))
                for kc in range(4):
                    cx.mm(pB[0:ncol, :], lb[:, lcol + kc * ncol:lcol + (kc + 1) * ncol], cl[:, kc, :], kc == 0, kc == 3,
                          (("c_l", kc),), (kB,))
                cx.act(dst[0:ncol, tsl], pB[0:ncol, :], func, (kB,), (("r_d", li, t4),))
    cx.barrier()

    with contextlib.ExitStack() as rsc:
        def f32(name, w=TS):
            return sb(name, [128, w], F32, rsc)
        R = [f32("r_R%d" % j, TS + 1) for j in range(3)]
        Xs = [f32("r_X%d" % j) for j in range(3)]
        tmpm = f32("r_tmpm")
        Gext = f32("r_Gext", TS + 1)
        negG0 = sb("r_negG0", [128, NCH], F32, rsc)
        Ein, Einv, Eex = f32("r_Ein"), f32("r_Einv"), f32("r_Eex")
        alr, gS, kk, kkn, kmod, tA, bon, yS, dd = (f32("r_alr"), f32("r_gS"), f32("r_kk"), f32("r_kkn"), f32("r_kmod"),
                                                   f32("r_tA"), f32("r_bon"), f32("r_yS"), f32("r_dd"))
        hb16 = sb("r_hb16", [128, TS], BF16, rsc)
        rt = sb("r_rt", [128, TS], BF16, rsc)
        BD = {n: sb("r_BD" + n, [128, NCH, 128], BF16, rsc) for n in "abkv"}
        Nk = sb("r_Nk", [128, 2, 512], BF16, rsc)
        Lk = sb("r_Lk", [128, 2, 512], BF16, rsc)
        Xk = sb("r_Xk", [128, 2, 512], BF16, rsc)
        NK = sb("r_NK", [128, 512], BF16, rsc)
        VB = sb("r_VB", [128, 512], BF16, rsc)
        BT = sb("r_BT", [128, 512], BF16, rsc)
        KT = sb("r_KT", [128, 512], BF16, rsc)
        MRB = sb("r_MRB", [128, 256], BF16, rsc)
        MRK = sb("r_MRK", [128, 256], BF16, rsc)
        H32 = f32("r_H32", 128)
        Ht = f32("r_Ht", 128)
        Hb = sb("r_Hb", [128, 2, 128], BF16, rsc)
        RHb = sb("r_RHb", [128, 128], BF16, rsc)
        Ub = sb("r_Ub", [128, 128], BF16, rsc)
        for n in "abkv":
            cx.memset(BD[n][:], 0.0, ("BD" + n,))
        cx.memset(Gext[:, 0:1], 0.0, ("r_Gext",))

        for hp in range(4):
            hcol = slice(hp * 128, (hp + 1) * 128)
            wt = [wtile() for _ in range(3)]
            cx.memset(H32[:], 0.0, ("r_H32",))
            cx.memset(Hb[:, 0, :], 0.0, (("r_Hb", 0),))
            for j in range(3):
                cx.memset(R[j][:, TS:TS + 1], 0.0, (("r_R", j),))
            hcur = 0
            for sbi in range(S // TS):
                t0 = sbi * TS
                tsl = slice(t0, t0 + TS)
                t4 = t0 // 512
                mk = ("m_mix", 4 + hp, t4)
                for j, (mucol, omcol) in enumerate(((P_MUR, 0), (P_MUK, 4), (P_MUV, 8))):
                    tw, kw = wt[j]
                    pA, kA = ps[j % 2], "ps%d" % (j % 2)
                    for k in range(8):
                        cx.mm(pA[:, 0:TS], tw[:, k * 128:(k + 1) * 128], hT[:, k, tsl], k == 0, k == 7, (kw, ("hT", k, t4)), (kA,))
                    cx.copy('dve', R[j][:, 0:1], R[j][:, TS:TS + 1], (("r_R", j),), (("r_R", j),))
                    cx.act(R[j][:, 1:TS + 1], pA[:, 0:TS], AF.Copy, (kA,), (("r_R", j),))
                    cx.ts(tmpm[:], R[j][:, 0:TS], pf[:, mucol + hp:mucol + hp + 1], None, ALU.mult, None, (("r_R", j),), ("r_tmpm",))
                    cx.stt(Xs[j][:], R[j][:, 1:TS + 1], pf2[:, omcol + hp:omcol + hp + 1], tmpm[:], ALU.mult, ALU.add,
                           (("r_R", j), "r_tmpm"), (("r_X", j),))
                rS, kS, vS = Xs
                pZ, kZ = ps[2], "ps2"
                cx.mm(pZ[:, 0:TS], lb[0:32, L_W2 + hp * 128:L_W2 + (hp + 1) * 128], dw[0:32, tsl], True, True, (("r_d", 0, t4),), (kZ,))
                cx.act(tA[:], pZ[:, 0:TS], AF.Sigmoid, (kZ,), ("r_tA",), bias=pf[:, P_W0 + hp:P_W0 + hp + 1])
                cx.ts(tA[:], tA[:], -0.6065306597126334, None, ALU.mult, None, ("r_tA",), ("r_tA",))
                cx.op('dve', lambda e: e.tensor_tensor_scan(out=Gext[:, 1:TS + 1], data0=tA[:], data1=tA[:], initial=0.0,
                                                            op0=ALU.add, op1=ALU.bypass), ("r_tA",), ("r_Gext",))
                cx.ts(negG0[:], Gext[:, 0:TS:CH], -1.0, None, ALU.mult, None, ("r_Gext",), ("r_negG0",))
                for c in range(NCH):
                    csl = slice(c * CH, (c + 1) * CH)
                    cx.act(Ein[:, csl], Gext[:, 1 + c * CH:1 + (c + 1) * CH], AF.Exp, ("r_Gext", "r_negG0"), ("r_Ein",),
                           bias=negG0[:, c:c + 1])
                    cx.act(Einv[:, csl], Gext[:, 1 + c * CH:1 + (c + 1) * CH], AF.Exp, ("r_Gext",), ("r_Einv",),
                           bias=Gext[:, c * CH:c * CH + 1], scale=-1.0)
                    cx.act(Eex[:, csl], Gext[:, c * CH:(c + 1) * CH], AF.Exp, ("r_Gext", "r_negG0"), ("r_Eex",),
                           bias=negG0[:, c:c + 1])
                pZ, kZ = ps[3], "ps3"
                cx.mm(pZ[:, 0:TS], lb[0:32, L_A2 + hp * 128:L_A2 + (hp + 1) * 128], da[0:32, tsl], True, True, (("r_d", 1, t4),), (kZ,))
                cx.act(alr[:], pZ[:, 0:TS], AF.Sigmoid, (kZ,), ("r_alr",), bias=pf[:, P_A0 + hp:P_A0 + hp + 1])
                pZ, kZ = ps[2], "ps2"
                cx.mm(pZ[:, 0:TS], lb[0:96, L_G2 + hp * 128:L_G2 + (hp + 1) * 128], dg[0:96, tsl], True, True, (("r_d", 2, t4),), (kZ,))
                cx.act(gS[:], pZ[:, 0:TS], AF.Copy, (kZ,), ("r_gS",))
                cx.ts(kk[:], kS[:], pf[:, P_KK + hp:P_KK + hp + 1], None, ALU.mult, None, (("r_X", 1),), ("r_kk",))
                cx.act(hb16[:], kk[:], AF.Square, ("r_kk",), ("r_hb16",))
                pZ, kZ = ps[3], "ps3"
                cx.mm(pZ[:, 0:TS], bdones, hb16[:], True, True, ("r_hb16",), (kZ,))
                cx.act(tA[:], pZ[:, 0:TS], AF.Ln, (kZ,), ("r_tA",), bias=1e-24)
                cx.act(tA[:], tA[:], AF.Exp, ("r_tA",), ("r_tA",), scale=-0.5)
                cx.tt(kkn[:], kk[:], tA[:], ALU.mult, ("r_kk", "r_tA"), ("r_kkn",))
                cx.ts(tA[:], alr[:], pf[:, P_KA + hp:P_KA + hp + 1], pf2[:, 24 + hp:25 + hp], ALU.mult, ALU.add, ("r_alr",), ("r_tA",))
                cx.tt(kmod[:], kS[:], tA[:], ALU.mult, (("r_X", 1), "r_tA"), ("r_kmod",))
                cx.stt(hb16[:], rS[:], pf[:, P_RK + hp:P_RK + hp + 1], kmod[:], ALU.mult, ALU.mult, (("r_X", 0), "r_kmod"), ("r_hb16",))
                pZ, kZ = ps[2], "ps2"
                cx.mm(pZ[:, 0:TS], bdones, hb16[:], True, True, ("r_hb16",), (kZ,))
                cx.tt(bon[:], pZ[:, 0:TS], vS[:], ALU.mult, (kZ, ("r_X", 2)), ("r_bon",))
                cx.tt(tA[:], kkn[:], alr[:], ALU.mult, ("r_kkn", "r_alr"), ("r_tA",))
                for e in range(2):
                    hs = slice(64 * e, 64 * e + 64)
                    cs = slice(64 * e, 64 * e + 64)

                    def v3(t):
                        return t[hs, :].rearrange("p (c t) -> p c t", t=CH)
                    cx.stt(BD["a"][hs, :, cs], v3(kkn), -1.0, v3(Eex), ALU.mult, ALU.mult, ("r_kkn", "r_Eex"), ("BDa",))
                    cx.tt(BD["b"][hs, :, cs], v3(tA), v3(Einv), ALU.mult, ("r_tA", "r_Einv"), ("BDb",))
                    cx.tt(BD["k"][hs, :, cs], v3(kmod), v3(Einv), ALU.mult, ("r_kmod", "r_Einv"), ("BDk",))
                    cx.copy('dve', BD["v"][hs, :, cs], v3(vS), (("r_X", 2),), ("BDv",))
                cx.tt(rt[:], rS[:], Ein[:], ALU.mult, (("r_X", 0), "r_Ein"), ("r_rt",))
                if hp == 0 and sbi == 0:
                    tap("rt0", rt[:], ("r_rt",))
                    tap("bda0", BD["a"][:], ("BDa",))
                def bank4(pb, kb, lh, lkey, rh, rkey):
                    for c in range(NCH):
                        cx.mm(pb[:, c * 128:(c + 1) * 128], lh(c), rh(c), True, True, (lkey, rkey), (kb,))
                bda = lambda c: BD["a"][:, c, :]
                bdb = lambda c: BD["b"][:, c, :]
                bdk = lambda c: BD["k"][:, c, :]
                bdv = lambda c: BD["v"][:, c, :]
                idf = lambda c: ident
                bank4(ps[0], "ps0", bdb, "BDb", bda, "BDa")
                cx.tt(Nk[:, 0, :], ps[0][:, :], SU4, ALU.mult, ("ps0",), (("r_Nk", 0),))
                bank4(ps[1], "ps1", bda, "BDa", bdb, "BDb")
                cx.tt(Lk[:, 0, :], ps[1][:, :], SL4, ALU.mult, ("ps1",), (("r_Lk", 0),))
                bank4(ps[2], "ps2", bdk, "BDk", bda, "BDa")
                cx.tt(NK[:], ps[2][:, :], SU4, ALU.mult, ("ps2",), ("r_NK",))
                bank4(ps[0], "ps0", bdv, "BDv", idf, "cb")
                cx.act(VB[:], ps[0][:, :], AF.Copy, ("ps0",), ("r_VB",))
                bank4(ps[1], "ps1", bdb, "BDb", idf, "cb")
                cx.act(BT[:], ps[1][:, :], AF.Copy, ("ps1",), ("r_BT",))
                bank4(ps[2], "ps2", bdk, "BDk", idf, "cb")
                cx.act(KT[:], ps[2][:, :], AF.Copy, ("ps2",), ("r_KT",))
                for c in range(NCH):
                    cx.mm(ps[5][:, c * CH:(c + 1) * CH], bdb(c), rt[:, c * CH:(c + 1) * CH], True, True, ("BDb", "r_rt"), ("ps5",))
                cx.tt(MRB[:], ps[5][:, 0:256], UI4, ALU.mult, ("ps5",), ("r_MRB",))
                for c in range(NCH):
                    cx.mm(ps[6][:, c * CH:(c + 1) * CH], bdk(c), rt[:, c * CH:(c + 1) * CH], True, True, ("BDk", "r_rt"), ("ps6",))
                cx.tt(MRK[:], ps[6][:, 0:256], UI4, ALU.mult, ("ps6",), ("r_MRK",))
                cx.tt(Xk[:, 0, :], Nk[:, 0, :], I4, ALU.add, (("r_Nk", 0),), (("r_Xk", 0),))
                cur = 0
                for lev in range(1, 6):
                    nxt = 1 - cur
                    Lc = lambda c, cur=cur: Lk[:, cur, c * 128:(c + 1) * 128]
                    Nc = lambda c, cur=cur: Nk[:, cur, c * 128:(c + 1) * 128]
                    Xc = lambda c, cur=cur: Xk[:, cur, c * 128:(c + 1) * 128]
                    bank4(ps[1], "ps1", Nc, ("r_Nk", cur), Lc, ("r_Lk", cur))
                    if lev < 5:
                        bank4(ps[0], "ps0", Lc, ("r_Lk", cur), Nc, ("r_Nk", cur))
                        cx.act(Nk[:, nxt, :], ps[0][:, :], AF.Copy, ("ps0",), (("r_Nk", nxt),))
                    cx.act(Lk[:, nxt, :], ps[1][:, :], AF.Copy, ("ps1",), (("r_Lk", nxt),))
                    Ln_ = lambda c, nxt=nxt: Lk[:, nxt, c * 128:(c + 1) * 128]
                    bank4(ps[2], "ps2", Ln_, ("r_Lk", nxt), Xc, ("r_Xk", cur))
                    cx.tt(Xk[:, nxt, :], ps[2][:, :], Xk[:, cur, :], ALU.add, ("ps2", ("r_Xk", cur)), (("r_Xk", nxt),))
                    cur = nxt
                XF = cur
                if hp == 0 and sbi == 0:
                    tap("xf0", Xk[:, XF, :], (("r_Xk", XF),))
                pY = ps[7]
                for c in range(NCH):
                    c128 = slice(c * 128, (c + 1) * 128)
                    c64 = slice(c * CH, (c + 1) * CH)
                    hn = 1 - hcur
                    pw = ps[3 + (c % 2)]
                    kwk = "ps%d" % (3 + (c % 2))
                    cx.mm(pw[:, 0:128], BD["a"][:, c, :], Hb[:, hcur, :], True, False, ("BDa", ("r_Hb", hcur)), (kwk,))
                    cx.mm(pw[:, 0:128], NK[:, c128], VB[:, c128], False, True, ("r_NK", "r_VB"), (kwk,))
                    cx.act(RHb[:], pw[:, 0:128], AF.Copy, (kwk,), ("r_RHb",))
                    cx.mm(pw[:, 128:256], Xk[:, XF, c128], RHb[:], True, True, (("r_Xk", XF), "r_RHb"), (kwk,))
                    cx.act(Ub[:], pw[:, 128:256], AF.Copy, (kwk,), ("r_Ub",))
                    cx.mm(pY[:, c64], Hb[:, hcur, :], rt[:, c64], True, False, (("r_Hb", hcur), "r_rt"), ("ps7",))
                    cx.mm(pY[:, c64], Ub[:], MRB[:, c64], False, False, ("r_Ub", "r_MRB"), ("ps7",))
                    cx.mm(pY[:, c64], VB[:, c128], MRK[:, c64], False, True, ("r_VB", "r_MRK"), ("ps7",))
                    cx.mm(pw[:, 256:384], BT[:, c128], Ub[:], True, False, ("r_BT", "r_Ub"), (kwk,))
                    cx.mm(pw[:, 256:384], KT[:, c128], VB[:, c128], False, True, ("r_KT", "r_VB"), (kwk,))
                    cx.tt(Ht[:], pw[:, 256:384], H32[:], ALU.add, (kwk, "r_H32"), ("r_Ht",))
                    gam = Ein[:, c * CH + CH - 1:c * CH + CH]
                    cx.act(Hb[:, hn, :], Ht[:], AF.Copy, ("r_Ht", "r_Ein"), (("r_Hb", hn),), scale=gam)
                    cx.ts(H32[:], Ht[:], gam, None, ALU.mult, None, ("r_Ht", "r_Ein"), ("r_H32",))
                    hcur = hn
                cx.act(yS[:], pY[:, 0:TS], AF.Copy, ("ps7",), ("r_yS",))
                if hp == 0 and sbi == 0:
                    tap("y0", yS[:], ("r_yS",))
                cx.copy('dve', hb16[:], yS[:], ("r_yS",), ("r_hb16",))
                pZ, kZ = ps[0], "ps0"
                cx.mm(pZ[:, 0:TS], bdones, hb16[:], True, True, ("r_hb16",), (kZ,))
                cx.stt(dd[:], pZ[:, 0:TS], -1.0 / 64, yS[:], ALU.mult, ALU.add, (kZ, "r_yS"), ("r_dd",))
                cx.act(hb16[:], dd[:], AF.Square, ("r_dd",), ("r_hb16",))
                pZ, kZ = ps[1], "ps1"
                cx.mm(pZ[:, 0:TS], bdones, hb16[:], True, True, ("r_hb16",), (kZ,))
                cx.act(tA[:], pZ[:, 0:TS], AF.Ln, (kZ,), ("r_tA",), bias=GN_EPS, scale=1.0 / 64)
                cx.act(tA[:], tA[:], AF.Exp, ("r_tA",), ("r_tA",), scale=-0.5)
                cx.tt(dd[:], dd[:], tA[:], ALU.mult, ("r_dd", "r_tA"), ("r_dd",))
                cx.ts(dd[:], dd[:], pf[:, P_LNW + hp:P_LNW + hp + 1], pf[:, P_LNB + hp:P_LNB + hp + 1], ALU.mult, ALU.add, ("r_dd",), ("r_dd",))
                cx.tt(dd[:], dd[:], bon[:], ALU.add, ("r_dd", "r_bon"), ("r_dd",))
                cx.tt(mix[:, 4 + hp, tsl], dd[:], gS[:], ALU.mult, ("r_dd", "r_gS"), (mk,))
        tap("rw", mix[:, 4:8, :], tuple(("m_mix", k, t) for k in range(4, 8) for t in range(4)))
    cx.barrier()


def _colchunk(W, c):
    return np.ascontiguousarray(W[:, 128 * c:128 * (c + 1)].reshape(8, 128, 128).transpose(1, 0, 2).reshape(128, 1024))


def _wstream(inp):
    tiles = []

    def ffn_tiles(wg, wu, wd):
        for (f0, nf) in FGROUPS:
            for c in range(nf):
                tiles.append(_colchunk(wg, f0 + c))
                tiles.append(_colchunk(wu, f0 + c))
            for c in range(nf):
                tiles.append(np.ascontiguousarray(wd[128 * (f0 + c):128 * (f0 + c + 1), :]))
    ffn_tiles(inp['ffn1_w_gate'][0], inp['ffn1_w_up'][0], inp['ffn1_w_down'][0])
    win = inp['w_in'][0]
    for hp in range(4):
        for j in range(3):
            tiles.append(_colchunk(win, 4 * j + hp))
    for kc in range(4):
        tiles.append(_colchunk(win, 24 + kc))
    for hp in range(4):
        for j in range(3):
            tiles.append(_colchunk(win, 12 + 4 * j + hp))
    for dc in range(8):
        tiles.append(_colchunk(inp['w_out'][0], dc))
    ffn_tiles(inp['ffn2_w_gate'][0], inp['ffn2_w_up'][0], inp['ffn2_w_down'][0])
    return tiles


def _consts():
    c = np.zeros((128, NC_COLS), np.float32)
    idx = np.arange(128)
    c[:, C_ID:C_ID + 128] = np.eye(128)
    c[:, C_ONES:C_ONES + 128] = 1.0
    h = idx // 64
    l = idx % 64
    same = (h[:, None] == h[None, :])
    c[:, C_BD:C_BD + 128] = same
    k = idx[:, None]
    q = idx[None, :]
    NEG = -30000.0
    c[:, C_MB:C_MB + 128] = np.where(q <= k, 0.0, NEG)
    c[:, C_MB + 128:C_MB + 256] = np.where(q >= k, 0.0, NEG)
    su = (same & (l[:, None] < l[None, :])).astype(np.float32)
    sl = (same & (l[:, None] > l[None, :])).astype(np.float32)
    for r in range(4):
        c[:, C_SU + 128 * r:C_SU + 128 * (r + 1)] = su
        c[:, C_SL + 128 * r:C_SL + 128 * (r + 1)] = sl
        c[:, C_I4 + 128 * r:C_I4 + 128 * (r + 1)] = np.eye(128)
        c[:, C_UI + 64 * r:C_UI + 64 * (r + 1)] = (l[:, None] <= np.arange(64)[None, :])
    return c


def _params(inp):
    p = np.zeros((128, NP_COLS), np.float32)

    def col8(v):
        return np.asarray(v).reshape(8, 128).T

    def col4(v):
        return np.asarray(v).reshape(4, 128).T
    p[:, P_G1:P_G1 + 8] = col8(inp['ffn1_norm'][0])
    p[:, P_GM:P_GM + 8] = col8(inp['mix_norm'][0])
    p[:, P_G2:P_G2 + 8] = col8(inp['ffn2_norm'][0])
    p[:, P_QG] = np.tile(np.asarray(inp['q_norm'][0]), 2)
    p[:, P_KG] = np.tile(np.asarray(inp['k_norm'][0]), 2)
    for col, name in ((P_MUR, 'mu_r'), (P_MUK, 'mu_k'), (P_MUV, 'mu_v'), (P_MUW, 'mu_w'), (P_MUA, 'mu_a'), (P_MUG, 'mu_g'),
                      (P_W0, 'w0'), (P_A0, 'a0'), (P_KK, 'k_k'), (P_KA, 'k_a'), (P_LNW, 'ln_x_w'), (P_LNB, 'ln_x_b')):
        p[:, col:col + 4] = col4(inp[name][0])
    p[:, P_RK:P_RK + 4] = col4(np.asarray(inp['r_k'][0]).reshape(512))
    return p


def _lora(inp):
    l = np.zeros((128, NL_COLS), np.float32)
    for col, name, n in ((L_W1, 'w1', 32), (L_A1, 'a1', 32), (L_G1, 'g1', 96)):
        w = np.asarray(inp[name][0])
        l[:, col:col + 4 * n] = w.reshape(4, 128, n).transpose(1, 0, 2).reshape(128, 4 * n)
    l[0:32, L_W2:L_W2 + 512] = np.asarray(inp['w2'][0])
    l[0:32, L_A2:L_A2 + 512] = np.asarray(inp['a2'][0])
    l[0:96, L_G2:L_G2 + 512] = np.asarray(inp['g2'][0])
    return l


def make_in_maps(inputs, ncores=8):
    inp = {k: np.asarray(v) for k, v in inputs.items()}
    wst = np.stack(_wstream(inp)).astype(np.float32)
    assert wst.shape[0] == NTILES, wst.shape
    consts = _consts()
    params = _params(inp)
    lora = _lora(inp)
    x = inp['x']
    maps = []
    for b in range(ncores):
        maps.append({"xT": np.ascontiguousarray(x[b].T), "wst": wst, "consts": consts, "params": params, "lora": lora})
    return maps


def kernel(**inputs):
    nc = build_nc()
    in_maps = make_in_maps(inputs, 8)
    res = run_bass_kernel_spmd(nc, in_maps, core_ids=list(range(8)))
    out = np.stack([np.ascontiguousarray(np.asarray(r["yT"]).T) for r in res.results], axis=0)
    return out.astype(np.float32)
```
